# Optimizing a Trainium2 kernel written in Bass

```python
import math
import jax, jax.numpy as jnp
from jax import lax
import numpy as np

D_MODEL = 2048
BATCH = 1
SEQ = 16384
DEPTH = 2
DEC_BATCH = 4
DEC_SEQ = 4096
PAST_LEN = 128

MIX_WIDTH = D_MODEL
N_ATTN_HEADS = 8
ATTN_HEAD_DIM = 64
ATTN_WIDTH = N_ATTN_HEADS * 2 * ATTN_HEAD_DIM
SSM_WIDTH = MIX_WIDTH - ATTN_WIDTH
SSM_HEAD_DIM = 64
N_SSM_HEADS = SSM_WIDTH // SSM_HEAD_DIM
N_SSM_GROUPS = 4
SSM_HPG = N_SSM_HEADS // N_SSM_GROUPS
D_STATE = 128
D_CONV = 5
CHUNK = 128
CONV_CH = SSM_WIDTH + 2 * N_SSM_GROUPS * D_STATE
OFF_Q = 0
OFF_K = OFF_Q + ATTN_WIDTH
OFF_V = OFF_K + ATTN_WIDTH
OFF_Z = OFF_V + ATTN_WIDTH
OFF_XBC = OFF_Z + SSM_WIDTH
OFF_DT = OFF_XBC + CONV_CH
IN_W = OFF_DT + 2 * N_SSM_HEADS
D_FF = 4 * D_MODEL
NUM_BUCKETS = 32
MAX_DISTANCE = 128
Q_BLOCK = 128
EPS = 1e-6

kernel_name = "hybrid_diffattn_ssd_encoder"


def rms_norm(x, w):
    xf = x.astype(jnp.float32)
    y = xf * lax.rsqrt(jnp.mean(xf * xf, axis=-1, keepdims=True) + EPS)
    return (y * w.astype(jnp.float32)).astype(x.dtype)


def rel_bucket(rel):
    nb = NUM_BUCKETS // 2
    max_exact = nb // 2
    ret = jnp.where(rel > 0, nb, 0)
    n = jnp.abs(rel)
    nf = jnp.maximum(n, 1).astype(jnp.float32)
    large = max_exact + (jnp.log(nf / max_exact) / math.log(MAX_DISTANCE / max_exact)
                         * (nb - max_exact)).astype(jnp.int32)
    large = jnp.minimum(large, nb - 1)
    return ret + jnp.where(n < max_exact, n, large)


def diff_attention(q, k, v, lam, lambda_init, subln_w, rel_table):
    bsz, seq = q.shape[0], q.shape[1]
    nq = seq // Q_BLOCK
    qb = (q * (ATTN_HEAD_DIM ** -0.5)).reshape(bsz, nq, Q_BLOCK, N_ATTN_HEADS, 2, ATTN_HEAD_DIM)
    qb = jnp.moveaxis(qb, 1, 0)
    k_pos = jnp.arange(seq)

    def block(args):
        q_blk, i = args
        logits = jnp.einsum("bqhcd,bkhcd->bchqk", q_blk, k).astype(jnp.float32)
        q_pos = i * Q_BLOCK + jnp.arange(Q_BLOCK)
        bucket = rel_bucket(k_pos[None, :] - q_pos[:, None])
        bias = jnp.transpose(rel_table[bucket], (2, 0, 1)).astype(jnp.float32)
        p = jax.nn.softmax(logits + bias[None, None], axis=-1)
        w = p[:, 0] - lam * p[:, 1]
        return jnp.einsum("bhqk,bkhe->bqhe", w.astype(v.dtype), v)

    out = lax.map(block, (qb, jnp.arange(nq)))
    out = jnp.moveaxis(out, 0, 1).reshape(bsz, seq, N_ATTN_HEADS, 2 * ATTN_HEAD_DIM)
    out = rms_norm(out, subln_w) * (1.0 - lambda_init)
    return out.reshape(bsz, seq, ATTN_WIDTH)


def depthwise_conv(x, w, b):
    y = lax.conv_general_dilated(x, w[:, None, :].astype(x.dtype), window_strides=(1,),
                                 padding=[(D_CONV // 2, D_CONV // 2)],
                                 dimension_numbers=("NWC", "WIO", "NWC"),
                                 feature_group_count=x.shape[-1])
    return y + b.astype(x.dtype)


def segsum_exp(a):
    cs = jnp.cumsum(a, axis=-1)
    diff = cs[..., :, None] - cs[..., None, :]
    mask = jnp.tril(jnp.ones((a.shape[-1], a.shape[-1]), dtype=bool))
    return jnp.exp(jnp.where(mask, diff, -jnp.inf))


def ssd_scan(x, dt, A, Bm, Cm):
    bsz, seq = x.shape[0], x.shape[1]
    nc = seq // CHUNK
    f32 = jnp.float32
    xc = x.astype(f32).reshape(bsz, nc, CHUNK, N_SSM_GROUPS, SSM_HPG, SSM_HEAD_DIM)
    Bc = Bm.astype(f32).reshape(bsz, nc, CHUNK, N_SSM_GROUPS, D_STATE)
    Cc = Cm.astype(f32).reshape(bsz, nc, CHUNK, N_SSM_GROUPS, D_STATE)
    dtc = dt.reshape(bsz, nc, CHUNK, N_SSM_GROUPS, SSM_HPG)
    a = jnp.moveaxis(dtc * A, 2, -1)
    a_cs = jnp.cumsum(a, axis=-1)
    xdt = xc * dtc[..., None]
    decay = segsum_exp(a)
    cb = jnp.einsum("bclgn,bcsgn->bcgls", Cc, Bc)
    y_diag = jnp.einsum("bcgls,bcghls,bcsghp->bclghp", cb, decay, xdt)
    decay_to_end = jnp.exp(a_cs[..., -1:] - a_cs)
    states = jnp.einsum("bcsgn,bcghs,bcsghp->bcghpn", Bc, decay_to_end, xdt)
    chunk_decay = jnp.exp(a_cs[..., -1])

    def step(h, inp):
        s, d = inp
        return h * d[..., None, None] + s, h

    h0 = jnp.zeros((bsz, N_SSM_GROUPS, SSM_HPG, SSM_HEAD_DIM, D_STATE), f32)
    _, prev = lax.scan(step, h0, (jnp.moveaxis(states, 1, 0), jnp.moveaxis(chunk_decay, 1, 0)))
    prev = jnp.moveaxis(prev, 0, 1)
    y_off = jnp.einsum("bclgn,bcghpn,bcghl->bclghp", Cc, prev, jnp.exp(a_cs))
    return (y_diag + y_off).reshape(bsz, seq, N_SSM_GROUPS, SSM_HPG, SSM_HEAD_DIM)


def ssd_mixer(z, xbc, dt_raw, conv_w, conv_b, dt_bias_f, dt_bias_b, a_log_f, a_log_b, d_skip, norm_w):
    bsz, seq = z.shape[0], z.shape[1]
    f32 = jnp.float32
    xbc = jax.nn.silu(depthwise_conv(xbc, conv_w, conv_b))
    xs = xbc[..., :SSM_WIDTH].reshape(bsz, seq, N_SSM_GROUPS, SSM_HPG, SSM_HEAD_DIM)
    Bm = xbc[..., SSM_WIDTH:SSM_WIDTH + N_SSM_GROUPS * D_STATE].reshape(bsz, seq, N_SSM_GROUPS, D_STATE)
    Cm = xbc[..., SSM_WIDTH + N_SSM_GROUPS * D_STATE:].reshape(bsz, seq, N_SSM_GROUPS, D_STATE)
    dt_raw = dt_raw.astype(f32)
    dt_f = jax.nn.softplus(dt_raw[..., :N_SSM_HEADS] + dt_bias_f.astype(f32)).reshape(bsz, seq, N_SSM_GROUPS, SSM_HPG)
    dt_b = jax.nn.softplus(dt_raw[..., N_SSM_HEADS:] + dt_bias_b.astype(f32)).reshape(bsz, seq, N_SSM_GROUPS, SSM_HPG)
    A_f = -jnp.exp(a_log_f.astype(f32)).reshape(N_SSM_GROUPS, SSM_HPG)
    A_b = -jnp.exp(a_log_b.astype(f32)).reshape(N_SSM_GROUPS, SSM_HPG)
    y_f = ssd_scan(xs, dt_f, A_f, Bm, Cm)
    flip = lambda t: jnp.flip(t, axis=1)
    y_b = flip(ssd_scan(flip(xs), flip(dt_b), A_b, flip(Bm), flip(Cm)))
    y = y_f + y_b + d_skip.astype(f32).reshape(N_SSM_GROUPS, SSM_HPG)[..., None] * xs.astype(f32)
    y = y.reshape(bsz, seq, SSM_WIDTH) * jax.nn.silu(z.astype(f32))
    y = rms_norm(y.reshape(bsz, seq, N_SSM_GROUPS, SSM_WIDTH // N_SSM_GROUPS),
                 norm_w.reshape(N_SSM_GROUPS, SSM_WIDTH // N_SSM_GROUPS))
    return y.reshape(bsz, seq, SSM_WIDTH).astype(z.dtype)


def trunk(x, params):
    (rel_bias, pre_norm_mix, w_in, lambda_q1, lambda_k1, lambda_q2, lambda_k2, attn_norm,
     conv_w, conv_b, dt_bias_fwd, dt_bias_bwd, a_log_fwd, a_log_bwd, d_skip, ssm_norm,
     w_out, post_norm_mix, pre_norm_mlp, w_up, w_down, post_norm_mlp) = params
    bsz, seq = x.shape[0], x.shape[1]
    for i in range(DEPTH):
        lambda_init = 0.8 - 0.6 * math.exp(-0.3 * i)
        h = rms_norm(x, pre_norm_mix[i])
        p = h @ w_in[i]
        q = p[..., OFF_Q:OFF_K].reshape(bsz, seq, N_ATTN_HEADS, 2, ATTN_HEAD_DIM)
        k = p[..., OFF_K:OFF_V].reshape(bsz, seq, N_ATTN_HEADS, 2, ATTN_HEAD_DIM)
        v = p[..., OFF_V:OFF_Z].reshape(bsz, seq, N_ATTN_HEADS, 2 * ATTN_HEAD_DIM)
        lam = (jnp.exp(jnp.sum(lambda_q1[i].astype(jnp.float32) * lambda_k1[i].astype(jnp.float32)))
               - jnp.exp(jnp.sum(lambda_q2[i].astype(jnp.float32) * lambda_k2[i].astype(jnp.float32)))
               + lambda_init)
        a_out = diff_attention(q, k, v, lam, lambda_init, attn_norm[i], rel_bias)
        s_out = ssd_mixer(p[..., OFF_Z:OFF_XBC], p[..., OFF_XBC:OFF_DT], p[..., OFF_DT:],
                          conv_w[i], conv_b[i], dt_bias_fwd[i], dt_bias_bwd[i],
                          a_log_fwd[i], a_log_bwd[i], d_skip[i], ssm_norm[i])
        mix = jnp.concatenate([a_out, s_out], axis=-1) @ w_out[i]
        x = x + rms_norm(mix, post_norm_mix[i])
        h = rms_norm(x, pre_norm_mlp[i])
        u = jnp.square(jax.nn.relu(h @ w_up[i]))
        x = x + rms_norm(u @ w_down[i], post_norm_mlp[i])
    return x


def setup_inputs(seed: int = 0) -> dict:
    key = jax.random.key(seed)
    ks = jax.random.split(key, 24)
    f32 = jnp.float32

    def nrm(k, shape, scale):
        return jax.random.normal(k, shape, f32) * scale

    def gain(k, shape):
        return 1.0 + 0.01 * jax.random.normal(k, shape, f32)

    dt = jnp.exp(jax.random.uniform(ks[12], (2, DEPTH, N_SSM_HEADS), f32, math.log(1e-3), math.log(1e-1)))
    dt_bias = dt + jnp.log(-jnp.expm1(-dt))
    a_log = jnp.log(jax.random.uniform(ks[13], (2, DEPTH, N_SSM_HEADS), f32, 1.0, 16.0))
    return {
        "x_prompt": nrm(ks[0], (BATCH, SEQ, D_MODEL), 1.0),
        "x_sample": nrm(ks[1], (DEC_BATCH, DEC_SEQ, D_MODEL), 1.0),
        "rel_bias": nrm(ks[2], (NUM_BUCKETS, N_ATTN_HEADS), 0.5),
        "pre_norm_mix": gain(ks[3], (DEPTH, D_MODEL)),
        "w_in": nrm(ks[4], (DEPTH, D_MODEL, IN_W), D_MODEL ** -0.5),
        "lambda_q1": nrm(ks[5], (DEPTH, ATTN_HEAD_DIM), 0.1),
        "lambda_k1": nrm(ks[6], (DEPTH, ATTN_HEAD_DIM), 0.1),
        "lambda_q2": nrm(ks[7], (DEPTH, ATTN_HEAD_DIM), 0.1),
        "lambda_k2": nrm(ks[8], (DEPTH, ATTN_HEAD_DIM), 0.1),
        "attn_norm": gain(ks[9], (DEPTH, 2 * ATTN_HEAD_DIM)),
        "conv_w": nrm(ks[10], (DEPTH, D_CONV, CONV_CH), D_CONV ** -0.5),
        "conv_b": nrm(ks[11], (DEPTH, CONV_CH), 0.01),
        "dt_bias_fwd": dt_bias[0],
        "dt_bias_bwd": dt_bias[1],
        "a_log_fwd": a_log[0],
        "a_log_bwd": a_log[1],
        "d_skip": gain(ks[14], (DEPTH, N_SSM_HEADS)),
        "ssm_norm": gain(ks[15], (DEPTH, SSM_WIDTH)),
        "w_out": nrm(ks[16], (DEPTH, MIX_WIDTH, D_MODEL), MIX_WIDTH ** -0.5),
        "post_norm_mix": gain(ks[17], (DEPTH, D_MODEL)),
        "pre_norm_mlp": gain(ks[18], (DEPTH, D_MODEL)),
        "w_up": nrm(ks[19], (DEPTH, D_MODEL, D_FF), D_MODEL ** -0.5),
        "w_down": nrm(ks[20], (DEPTH, D_FF, D_MODEL), D_FF ** -0.5),
        "post_norm_mlp": gain(ks[21], (DEPTH, D_MODEL)),
    }


def reference(x_prompt, x_sample, rel_bias, pre_norm_mix, w_in, lambda_q1, lambda_k1, lambda_q2,
              lambda_k2, attn_norm, conv_w, conv_b, dt_bias_fwd, dt_bias_bwd, a_log_fwd, a_log_bwd,
              d_skip, ssm_norm, w_out, post_norm_mix, pre_norm_mlp, w_up, w_down, post_norm_mlp):
    params = (rel_bias, pre_norm_mix, w_in, lambda_q1, lambda_k1, lambda_q2, lambda_k2, attn_norm,
              conv_w, conv_b, dt_bias_fwd, dt_bias_bwd, a_log_fwd, a_log_bwd, d_skip, ssm_norm,
              w_out, post_norm_mix, pre_norm_mlp, w_up, w_down, post_norm_mlp)
    y_prompt = trunk(x_prompt, params)
    y_sample = trunk(x_sample, params)
    return (y_prompt, y_sample)
```

```python
import contextlib
import math
import numpy as np
import ml_dtypes
import concourse.bass as bass
import concourse.mybir as mybir
from concourse.bass_utils import run_bass_kernel_spmd

F32 = mybir.dt.float32
BF16 = mybir.dt.bfloat16
U8 = mybir.dt.uint8
AF = mybir.ActivationFunctionType
ALU = mybir.AluOpType
AX = mybir.AxisListType

D = 2048
NK = 16
INW = 6176
DFF = 8192
NH = 8
EPS = 1e-6
DEPTH = 2
C0 = 640
GW = 1376
GV = GW + 127
GVP = 1536
NEG = -200.0
SBUF_BYTES = 192 * 1024

ENGS = ("pe", "act", "dve", "pool", "sp")


class Rec:
    __slots__ = ("eng", "fn", "deps", "dmawaits", "sem", "inc", "count", "needed", "is_dma", "idx")

    def __init__(self, eng, fn):
        self.eng = eng
        self.fn = fn
        self.deps = []
        self.dmawaits = {}
        self.sem = None
        self.inc = 0
        self.count = 0
        self.needed = False
        self.is_dma = False
        self.idx = 0


class Sched:
    def __init__(self, nc):
        self.nc = nc
        self.streams = {e: [] for e in ENGS}
        self.last_writer = {}
        self.readers = {}
        self.dma_total = {}
        self.dma_sems = {}
        self.pending = {e: [] for e in ENGS}
        self.prev_marker = {}

    def _dep(self, rec, prod):
        if prod is None or prod is rec:
            return
        if prod.is_dma:
            rec.dmawaits[prod.sem] = max(rec.dmawaits.get(prod.sem, 0), self.dma_total[prod.sem])
        else:
            if prod.eng == "pe" and rec.eng == "pe":
                return
            cur = rec.deps.get(prod.eng)
            if cur is None or prod.idx > cur.idx:
                rec.deps[prod.eng] = prod

    def _finish(self, rec):
        for p in rec.deps.values():
            p.needed = True
        rec.deps = list(rec.deps.values())

    def add(self, eng, fn, reads=(), writes=(), dma_sem=None, inc=16):
        rec = Rec(eng, fn)
        rec.deps = {}
        rec.idx = len(self.streams[eng])
        if self.pending[eng]:
            for p in self.pending[eng]:
                if p.eng != eng:
                    self._dep(rec, p)
            self.pending[eng] = []
        for k in reads:
            self._dep(rec, self.last_writer.get(k))
        for k in writes:
            self._dep(rec, self.last_writer.get(k))
            for r in self.readers.get(k, ()):
                self._dep(rec, r)
        self._finish(rec)
        if dma_sem is not None:
            rec.is_dma = True
            rec.sem = dma_sem
            rec.inc = inc
            self.dma_total[dma_sem] = self.dma_total.get(dma_sem, 0) + inc
        for k in reads:
            self.readers.setdefault(k, []).append(rec)
        for k in writes:
            self.last_writer[k] = rec
            self.readers[k] = []
        self.streams[eng].append(rec)
        return rec

    def barrier(self, markers):
        recs = []
        for e in ("act", "dve", "pool"):
            r = Rec(e, markers[e])
            r.deps = {}
            r.idx = len(self.streams[e])
            if self.pending[e]:
                for p in self.pending[e]:
                    if p.eng != e:
                        self._dep(r, p)
                self.pending[e] = []
            self._dep(r, self.last_writer.get("dmy"))
            self._dep(r, self.prev_marker.get(e))
            self.prev_marker[e] = r
            self._finish(r)
            if e == "pool":
                r.dmawaits = dict(self.dma_total)
            r.needed = True
            self.streams[e].append(r)
            recs.append(r)
        for e in ENGS:
            self.pending[e] = list(recs)
        self.last_writer = {}
        self.readers = {}

    def emit(self, final_wait_sems=()):
        nc = self.nc
        for e in ENGS:
            c = 0
            for r in self.streams[e]:
                if r.needed and not r.is_dma:
                    c += 1
                    r.count = c
        with contextlib.ExitStack() as st:
            eng_sem = {e: st.enter_context(nc.semaphore("s_" + e)) for e in ENGS}
            for name in self.dma_total:
                self.dma_sems[name] = st.enter_context(nc.semaphore("d_" + name))
            block = st.enter_context(nc.Block())
            streams, dma_sems, dma_total = self.streams, self.dma_sems, self.dma_total

            def run(e, eng):
                waited = {}
                for r in streams[e]:
                    for p in r.deps:
                        key = ("e", p.eng)
                        if waited.get(key, 0) < p.count:
                            eng.wait_ge(eng_sem[p.eng], p.count)
                            waited[key] = p.count
                    for s, v in r.dmawaits.items():
                        key = ("d", s)
                        if v > 0 and waited.get(key, 0) < v:
                            eng.wait_ge(dma_sems[s], v)
                            waited[key] = v
                    ins = r.fn(eng)
                    if r.is_dma:
                        ins.then_inc(dma_sems[r.sem], r.inc)
                    elif r.needed:
                        ins.then_inc(eng_sem[e], 1)
                if e == "sp":
                    for s in final_wait_sems:
                        eng.wait_ge(dma_sems[s], dma_total[s])

            @block.tensor
            def _(eng):
                run("pe", eng)

            @block.scalar
            def _(eng):
                run("act", eng)

            @block.vector
            def _(eng):
                run("dve", eng)

            @block.gpsimd
            def _(eng):
                run("pool", eng)

            @block.sync
            def _(eng):
                run("sp", eng)


class Arena:
    def __init__(self, t, nbytes):
        self.t = t
        self.nbytes = nbytes
        self.off = 0

    def reset(self, off=0):
        self.off = off

    def alloc(self, free_shape, dt):
        n = 1
        for s in free_shape:
            n *= s
        esz = 4 if dt == F32 else 2
        nb = (n * esz + 63) // 64 * 64
        assert self.off + nb <= self.nbytes, ("SBUF arena overflow", self.off, nb)
        v = self.t[:, self.off:self.off + nb].bitcast(dt)[:, 0:n]
        self.off += nb
        if len(free_shape) == 2:
            v = v.rearrange("p (a b) -> p a b", a=free_shape[0])
        elif len(free_shape) == 3:
            v = v.rearrange("p (a b c) -> p a b c", a=free_shape[0], b=free_shape[1])
        elif len(free_shape) == 4:
            v = v.rearrange("p (a b c d) -> p a b c d", a=free_shape[0], b=free_shape[1], c=free_shape[2])
        return v


def pc_layout(T):
    NT = T // 128
    NQB = T // 512
    off = {}
    o = 0
    for sg, R in ((0, 8), (1, 2)):
        for nm in ("selL", "selR", "stF", "stB"):
            off[(nm, sg)] = o
            o += R
        for nm in ("mL", "mR", "mN"):
            off[(nm, sg)] = o
            o += NQB * R * NT
    return off, o


def pc_values(T, core):
    NT = T // 128
    NQB = T // 512
    off, n = pc_layout(T)
    v = np.zeros((n,), np.float32)
    for sg, R in ((0, 8), (1, 2)):
        pos = core if sg == 0 else core % 2
        for r in range(R):
            v[off[("selL", sg)] + r] = 1.0 if r == pos - 1 else 0.0
            v[off[("selR", sg)] + r] = 1.0 if r == pos + 1 else 0.0
            v[off[("stF", sg)] + r] = 1.0 if r < pos else 0.0
            v[off[("stB", sg)] + r] = 1.0 if r > pos else 0.0
        NKT = R * NT
        for qb in range(NQB):
            Qt = pos * NT + 4 * qb
            for kt in range(NKT):
                o = kt - Qt
                i = qb * NKT + kt
                v[off[("mN", sg)] + i] = 1.0 if -1 <= o <= 4 else 0.0
                v[off[("mL", sg)] + i] = 1.0 if o <= -2 else 0.0
                v[off[("mR", sg)] + i] = 1.0 if o >= 5 else 0.0
    return np.broadcast_to(v[None, :], (128, n)).copy()


def rel_bucket_np(rel):
    nb = 16
    max_exact = 8
    rel = np.asarray(rel, np.int64)
    ret = np.where(rel > 0, nb, 0)
    n = np.abs(rel)
    nf = np.maximum(n, 1).astype(np.float32)
    large = max_exact + (np.log(nf / np.float32(max_exact)) / np.float32(math.log(128 / max_exact))
                         * np.float32(nb - max_exact)).astype(np.int32)
    large = np.minimum(large, nb - 1)
    return ret + np.where(n < max_exact, n, large)


class StopBuild(Exception):
    pass


def build_program(T, debug=False, stop=None):
    NT = T // 128
    NQB = T // 512
    RS = (8, 2)
    GROUPS = ([list(range(8))], [[0, 1], [2, 3], [4, 5], [6, 7]])
    pco, NPC = pc_layout(T)
    TOK = 2 * T
    TBLK = min(512, TOK)

    nc = bass.Bass("TRN2", target_bir_lowering=False)

    def din(name, shape, dt=F32):
        return nc.dram_tensor(name, list(shape), dt, kind="ExternalInput")

    xin = din("xin", [TOK, D])
    pc = din("pc", [128, NPC])
    cst = din("cst", [128, 512])
    relR = din("relR", [32, GVP])
    rel_bias = din("rel_bias", [32, 8])
    pre_norm_mix = din("pre_norm_mix", [DEPTH, D])
    w_in = din("w_in", [DEPTH, D, INW])
    lam_in = {n: din(n, [DEPTH, 64]) for n in ("lambda_q1", "lambda_k1", "lambda_q2", "lambda_k2")}
    attn_norm = din("attn_norm", [DEPTH, 128])
    conv_w = din("conv_w", [DEPTH, 5, 2048])
    conv_b = din("conv_b", [DEPTH, 2048])
    dt_bias_fwd = din("dt_bias_fwd", [DEPTH, 16])
    dt_bias_bwd = din("dt_bias_bwd", [DEPTH, 16])
    a_log_fwd = din("a_log_fwd", [DEPTH, 16])
    a_log_bwd = din("a_log_bwd", [DEPTH, 16])
    d_skip = din("d_skip", [DEPTH, 16])
    ssm_norm = din("ssm_norm", [DEPTH, 1024])
    w_out = din("w_out", [DEPTH, D, D])
    post_norm_mix = din("post_norm_mix", [DEPTH, D])
    pre_norm_mlp = din("pre_norm_mlp", [DEPTH, D])
    w_up = din("w_up", [DEPTH, D, DFF])
    w_down = din("w_down", [DEPTH, DFF, D])
    post_norm_mlp = din("post_norm_mlp", [DEPTH, D])
    yout = nc.dram_tensor("yout", [TOK, D], F32, kind="ExternalOutput")

    dbg_kind = dict(kind="ExternalOutput") if debug else {}

    def dscr(name, shape, dt=F32):
        return nc.dram_tensor(name, list(shape), dt, **dbg_kind)

    xa = dscr("xa", [TOK, D])
    xb = dscr("xb", [TOK, D])
    qT = [dscr(f"qT{s}", [1024, T], BF16) for s in range(2)]
    kT_loc = [nc.dram_tensor(f"kT_loc{s}", [1024, T], BF16) for s in range(2)]
    kT_all = [nc.dram_tensor(f"kT_all{s}", [RS[s] * 1024, T], BF16) for s in range(2)]
    v_loc = [nc.dram_tensor(f"v_loc{s}", [NH * 128, NT * 128], BF16) for s in range(2)]
    v_all = [nc.dram_tensor(f"v_all{s}", [RS[s] * NH * 128, NT * 128], BF16) for s in range(2)]
    zscr = [dscr(f"z{s}", [T, 1024]) for s in range(2)]
    xbcT = [dscr(f"xbcT{s}", [2048, T]) for s in range(2)]
    dtraw = [dscr(f"dtraw{s}", [T, 32]) for s in range(2)]
    edge_loc = [nc.dram_tensor(f"edge_loc{s}", [128, 64], F32) for s in range(2)]
    edge_all = [nc.dram_tensor(f"edge_all{s}", [RS[s] * 128, 64], F32) for s in range(2)]
    ao = [dscr(f"ao{s}", [T, 2048], BF16) for s in range(2)]
    Sscr = [nc.dram_tensor(f"Sscr{s}", [NT * 128, 2048], F32) for s in range(2)]
    Pscr = [nc.dram_tensor(f"Pscr{s}", [NT * 128, 2048], BF16) for s in range(2)]
    st_loc = [nc.dram_tensor(f"st_loc{s}", [128, 2048 + 32], F32) for s in range(2)]
    st_all = [nc.dram_tensor(f"st_all{s}", [RS[s] * 128, 2048 + 32], F32) for s in range(2)]
    gvec = nc.dram_tensor("gvec", [8, GVP], F32)
    gsk = nc.dram_tensor("gsk", [8 * 128, GVP], F32)

    S = Sched(nc)

    def bc_dram(handle, offset, n):
        return bass.AP(tensor=handle.ap().tensor, offset=offset, ap=[[0, 128], [1, n]])

    with contextlib.ExitStack() as st:
        arena_t = st.enter_context(nc.sbuf_tensor("arena", [128, SBUF_BYTES], U8))
        psum = st.enter_context(nc.psum_tensor("psum", [128, 8, 512], F32))
        AR = Arena(arena_t, SBUF_BYTES)

        def pbank(b, n=1):
            return psum[:, b:b + n, :].rearrange("p a b -> p (a b)")

        def mm(out, lhsT, rhs, start, stop, reads, writes, skip=False):
            if skip:
                S.add("pe", lambda e: e.matmul(out, lhsT=lhsT, rhs=rhs, start=start, stop=stop, skip_group_check=True), reads, writes)
            else:
                S.add("pe", lambda e: e.matmul(out, lhsT=lhsT, rhs=rhs, start=start, stop=stop), reads, writes)

        def tr(out, in_, ident, reads, writes):
            S.add("pe", lambda e: e.transpose(out=out, in_=in_, identity=ident), reads, writes)

        def act(out, in_, func, reads, writes, bias=None, scale=1.0, accum_out=None):
            kw = {}
            if bias is not None:
                kw["bias"] = bias
            if accum_out is not None:
                kw["accum_out"] = accum_out
            S.add("act", lambda e: e.activation(out=out, in_=in_, func=func, scale=scale, **kw), reads, writes)

        def ts(eng, out, in0, s1, s2, op0, op1, reads, writes):
            if op1 is None:
                S.add(eng, lambda e: e.tensor_scalar(out=out, in0=in0, scalar1=s1, scalar2=None, op0=op0), reads, writes)
            else:
                S.add(eng, lambda e: e.tensor_scalar(out=out, in0=in0, scalar1=s1, scalar2=s2, op0=op0, op1=op1), reads, writes)

        def stt(out, in0, scalar, in1, op0, op1, reads, writes):
            S.add("dve", lambda e: e.scalar_tensor_tensor(out=out, in0=in0, scalar=scalar, in1=in1, op0=op0, op1=op1), reads, writes)

        def tt(eng, out, in0, in1, op, reads, writes):
            S.add(eng, lambda e: e.tensor_tensor(out=out, in0=in0, in1=in1, op=op), reads, writes)

        def rsq(x, reads, writes):
            ncol = x.shape[1]
            S.add("pool", lambda e: e.tensor_tensor(out=x, in0=x, in1=neghalf[:, 0:ncol], op=ALU.pow), list(reads) + ["neghalf"], writes)

        def recip(out, in_, reads, writes):
            S.add("dve", lambda e: e.reciprocal(out=out, in_=in_), reads, writes)

        def cp(eng, out, in_, reads, writes):
            if eng == "act":
                S.add("act", lambda e: e.activation(out=out, in_=in_, func=AF.Copy), reads, writes)
            else:
                S.add(eng, lambda e: e.tensor_copy(out=out, in_=in_), reads, writes)

        def ms(eng, ap, val, writes):
            S.add(eng, lambda e: e.memset(ap, val), (), writes)

        def red(out, in_, reads, writes):
            S.add("dve", lambda e: e.tensor_reduce(out=out, in_=in_, axis=AX.X, op=ALU.add), reads, writes)

        def dma(q, out, in_, reads, writes, sem):
            S.add(q, lambda e: e.dma_start(out=out, in_=in_), reads, writes, dma_sem=sem)

        def dma_nc(q, out, in_, reads, writes, sem):
            def fn(e):
                with nc.allow_non_contiguous_dma(reason="tiny one-off strided gather"):
                    return e.dma_start(out=out, in_=in_)
            S.add(q, fn, reads, writes, dma_sem=sem)

        def coll_ap(in_ap, out_ap, groups, sem):
            S.add("pool", lambda e: e.collective_compute("AllGather", ALU.bypass, replica_groups=groups,
                                                         ins=[in_ap.opt()], outs=[out_ap.opt()]),
                  (), (), dma_sem=sem, inc=1)

        def coll(ins, outs, groups, sem):
            S.add("pool", lambda e: e.collective_compute("AllGather", ALU.bypass, replica_groups=groups,
                                                         ins=[ins.ap().opt()], outs=[outs.ap().opt()]),
                  (), (), dma_sem=sem, inc=1)

        cst_t = AR.alloc((512,), F32)
        ident_f = cst_t[:, 0:128]
        Umat = cst_t[:, 128:256]
        Lmat = cst_t[:, 256:384]
        ones_f = cst_t[:, 384:512]
        identb = AR.alloc((128,), BF16)
        pc_t = AR.alloc((NPC,), F32)
        dmy = AR.alloc((8,), F32)
        lamneg = AR.alloc((1,), F32)
        A_bc = AR.alloc((32,), F32)
        dtb_bc = AR.alloc((32,), F32)
        Dsk_bc = AR.alloc((16,), F32)
        anw = AR.alloc((128,), F32)
        cw = AR.alloc((16, 5), F32)
        cbias = AR.alloc((16,), F32)
        small = AR.alloc((64,), F32)
        neghalf = AR.alloc((8,), F32)
        PERSIST = AR.off

        markers = {
            "pe": lambda e: e.nop(),
            "sp": lambda e: e.nop(),
            "act": lambda e: e.activation(out=dmy[:, 0:1], in_=dmy[:, 1:2], func=AF.Copy),
            "dve": lambda e: e.memset(dmy[:, 2:3], 0.0),
            "pool": lambda e: e.memset(dmy[:, 4:5], 0.0),
        }

        stage_ctr = [0]

        def stage(name):
            stage_ctr[0] += 1
            if stop is not None and stage_ctr[0] == stop:
                print("STOP at stage", stage_ctr[0], name)
                raise StopBuild()

        def barrier():
            S.barrier(markers)
            AR.reset(PERSIST)

        def pcol(name, sg, i):
            o = pco[(name, sg)] + i
            return pc_t[:, o:o + 1]

        ms("dve", dmy, 0.0, ["dmy"])
        ms("dve", neghalf, -0.5, ["neghalf"])
        dma("sp", cst_t, cst.ap(), (), ["cst"], "cst")
        dma("sp", pc_t, pc.ap(), (), ["pc"], "pc")
        cp("dve", identb, ident_f, ["cst"], ["identb"])
        try:
            stage('setup0')
        except StopBuild:
            S.barrier(markers)
            S.emit(final_wait_sems=list(S.dma_total.keys()))
            return nc
        relb_t = AR.alloc((128,), F32)
        relR_t = AR.alloc((GVP,), F32)
        gv_t = AR.alloc((GVP,), F32)
        grep = AR.alloc((GVP,), F32)
        ms("dve", relb_t, 0.0, ["relb"])
        ms("pool", relR_t, 0.0, ["relR"])
        dma("sp", relb_t[0:32, 0:8], rel_bias.ap(), (), ["relb"], "relb")
        dma("sp", relR_t[0:32, :], relR.ap(), (), ["relR"], "relR")
        for i in range(3):
            mm(psum[:, i, :], relb_t, relR_t[:, i * 512:(i + 1) * 512], True, True, ["relb", "relR"], [("ps", i)])
            cp("dve", gv_t[:, i * 512:(i + 1) * 512], psum[:, i, :], [("ps", i)], ["gv"])
        dma("sp", gvec.ap(), gv_t[0:8, :], ["gv"], ["gvec"], "gv")
        try:
            stage('setup1')
        except StopBuild:
            S.barrier(markers)
            S.emit(final_wait_sems=list(S.dma_total.keys()))
            return nc
        for h in range(8):
            dma("sp", grep, bc_dram(gvec, h * GVP, GVP), ["gvec"], ["grep"], "grep")
            dma("sp", gsk.ap()[h * 128:(h + 1) * 128, :], grep, ["grep"], [], "grep")
        barrier()
        try:
            stage('setup2')
        except StopBuild:
            S.barrier(markers)
            S.emit(final_wait_sems=list(S.dma_total.keys()))
            return nc

        try:
            xsrc = xin
            for L in range(DEPTH):
                lambda_init = 0.8 - 0.6 * math.exp(-0.3 * L)
                xdst = yout if L == DEPTH - 1 else xb

                lt = AR.alloc((4, 64), F32)
                for i, nme in enumerate(("lambda_q1", "lambda_k1", "lambda_q2", "lambda_k2")):
                    dma("sp", lt[:, i, :], bc_dram(lam_in[nme], L * 64, 64), (), ["lt"], "lt")
                tt("dve", lt[:, 0, :], lt[:, 0, :], lt[:, 1, :], ALU.mult, ["lt"], ["lt"])
                tt("dve", lt[:, 2, :], lt[:, 2, :], lt[:, 3, :], ALU.mult, ["lt"], ["lt"])
                red(small[:, 0:1], lt[:, 0, :], ["lt"], ["sm0"])
                red(small[:, 1:2], lt[:, 2, :], ["lt"], ["sm0"])
                act(small[:, 2:4], small[:, 0:2], AF.Exp, ["sm0"], ["sm1"])
                tt("dve", small[:, 4:5], small[:, 3:4], small[:, 2:3], ALU.subtract, ["sm1"], ["sm2"])
                ts("dve", lamneg, small[:, 4:5], -lambda_init, None, ALU.add, None, ["sm2"], ["lamneg"])
                dma("sp", A_bc[:, 0:16], bc_dram(a_log_fwd, L * 16, 16), (), ["A0"], "prm")
                dma("sp", A_bc[:, 16:32], bc_dram(a_log_bwd, L * 16, 16), (), ["A0"], "prm")
                dma("sp", dtb_bc[:, 0:16], bc_dram(dt_bias_fwd, L * 16, 16), (), ["dtb"], "prm")
                dma("sp", dtb_bc[:, 16:32], bc_dram(dt_bias_bwd, L * 16, 16), (), ["dtb"], "prm")
                dma("sp", Dsk_bc, bc_dram(d_skip, L * 16, 16), (), ["Dsk"], "prm")
                dma("sp", anw, bc_dram(attn_norm, L * 128, 128), (), ["anw0"], "prm")
                for k in range(5):
                    dma_nc("sp", cw[:, :, k], conv_w.ap()[L][k].rearrange("(cb p) -> p cb", p=128), (), ["cw"], "prm")
                dma_nc("sp", cbias, conv_b.ap()[L].rearrange("(cb p) -> p cb", p=128), (), ["cbias"], "prm")
                act(A_bc, A_bc, AF.Exp, ["A0"], ["A1"])
                ts("dve", A_bc, A_bc, -1.0, None, ALU.mult, None, ["A1"], ["A"])
                ts("dve", anw, anw, 1.0 - lambda_init, None, ALU.mult, None, ["anw0"], ["anw"])
                barrier()

                stage('params')
                nw = AR.alloc((D,), F32)
                dma("sp", nw, bc_dram(pre_norm_mix, L * D, D), (), ["nw"], "nw")
                xt = [AR.alloc((D,), F32) for _ in range(2)]
                hb = [AR.alloc((D,), BF16) for _ in range(2)]
                hT = AR.alloc((NK, T), BF16)
                wblk = [AR.alloc((NK, 512), BF16) for _ in range(2)]
                stg_f = [AR.alloc((512,), F32) for _ in range(3)]
                stg_b = [AR.alloc((512,), BF16) for _ in range(3)]
                ssq = AR.alloc((4,), F32)
                pT = psum[:, 0:2, :].rearrange("p a b -> p (a b)").bitcast(BF16).rearrange("p (k t) -> p k t", k=NK)
                wcnt = 0
                scnt = 0
                obank = 0
                for sg in range(2):
                    for tix in range(NT):
                        sl = tix % 2
                        r0 = sg * T + tix * 128
                        dma("sp", xt[sl], xsrc.ap()[r0:r0 + 128, :], (), [("xt", sl)], f"xt{sl}")
                        act(hb[sl], xt[sl], AF.Square, [("xt", sl)], [("hb", sl), ("ssq", sl)], accum_out=ssq[:, sl:sl + 1])
                        ts("dve", ssq[:, 2 + sl:3 + sl], ssq[:, sl:sl + 1], 1.0 / D, EPS, ALU.mult, ALU.add, [("ssq", sl)], [("rs", sl)])
                        rsq(ssq[:, 2 + sl:3 + sl], [("rs", sl)], [("rs", sl)])
                        stt(hb[sl], xt[sl], ssq[:, 2 + sl:3 + sl], nw, ALU.mult, ALU.mult, [("xt", sl), ("rs", sl), "nw", ("hb", sl)], [("hb", sl)])
                        for k in range(NK):
                            tr(pT[:, k, :], hb[sl][:, k * 128:(k + 1) * 128], identb, [("hb", sl), "identb"], ["pT"])
                        cp("act" if tix % 2 else "dve", hT[:, :, tix * 128:(tix + 1) * 128], pT, ["pT"], ["hT"])
                    for j in range(13):
                        wsl = wcnt % 2
                        wcnt += 1
                        ncol = 512 if j < 12 else 32
                        dma("pool", wblk[wsl][:, :, 0:ncol],
                            w_in.ap()[L][:, j * 512:j * 512 + ncol].rearrange("(k p) n -> p k n", p=128),
                            (), [("w", wsl)], f"w{wsl}")
                        if j in (0, 1, 2, 3, 8, 9, 10, 11):
                            for cc in range(4):
                                for tb in range(T // 512):
                                    bk = 2 + obank % 6
                                    obank += 1
                                    for k in range(NK):
                                        mm(psum[:, bk, :], wblk[wsl][:, k, cc * 128:(cc + 1) * 128], hT[:, k, tb * 512:(tb + 1) * 512],
                                           k == 0, k == NK - 1, [("w", wsl), "hT"], [("ps", bk)])
                                    ss_ = scnt % 3
                                    scnt += 1
                                    ev = "act" if scnt % 2 else "dve"
                                    if j < 4:
                                        cp(ev, stg_b[ss_], psum[:, bk, :], [("ps", bk)], [("sb", ss_)])
                                        row = (j % 2) * 512 + cc * 128
                                        dst = (qT[sg] if j < 2 else kT_loc[sg]).ap()[row:row + 128, tb * 512:(tb + 1) * 512]
                                        dma("sp", dst, stg_b[ss_], [("sb", ss_)], [], f"sb{ss_}")
                                    else:
                                        cp(ev, stg_f[ss_], psum[:, bk, :], [("ps", bk)], [("sf", ss_)])
                                        cbk = (j - 8) * 4 + cc
                                        dma("sp", xbcT[sg].ap()[cbk * 128:(cbk + 1) * 128, tb * 512:(tb + 1) * 512], stg_f[ss_],
                                            [("sf", ss_)], [], f"sf{ss_}")
                                        if tb == 0:
                                            dma_nc("sp", edge_loc[sg].ap()[:, cbk * 4:cbk * 4 + 2], stg_f[ss_][:, 0:2], [("sf", ss_)], [], f"sf{ss_}")
                                        if tb == T // 512 - 1:
                                            dma_nc("sp", edge_loc[sg].ap()[:, cbk * 4 + 2:cbk * 4 + 4], stg_f[ss_][:, 510:512], [("sf", ss_)], [], f"sf{ss_}")
                        else:
                            for tix in range(NT):
                                bk = 2 + obank % 6
                                obank += 1
                                for k in range(NK):
                                    mm(psum[:, bk, 0:ncol], hT[:, k, tix * 128:(tix + 1) * 128], wblk[wsl][:, k, 0:ncol],
                                       k == 0, k == NK - 1, [("w", wsl), "hT"], [("ps", bk)])
                                ss_ = scnt % 3
                                scnt += 1
                                ev = "act" if scnt % 2 else "dve"
                                if j in (4, 5):
                                    cp(ev, stg_b[ss_], psum[:, bk, :], [("ps", bk)], [("sb", ss_)])
                                    vview = v_loc[sg].ap().rearrange("(h p) (kt e) -> p h kt e", p=128, e=128)
                                    h0 = (j - 4) * 4
                                    dma("sp", vview[:, h0:h0 + 4, tix, :], stg_b[ss_].rearrange("p (h e) -> p h e", h=4),
                                        [("sb", ss_)], [], f"sb{ss_}")
                                elif j in (6, 7):
                                    cp(ev, stg_f[ss_], psum[:, bk, :], [("ps", bk)], [("sf", ss_)])
                                    dma("sp", zscr[sg].ap()[tix * 128:(tix + 1) * 128, (j - 6) * 512:(j - 5) * 512], stg_f[ss_],
                                        [("sf", ss_)], [], f"sf{ss_}")
                                else:
                                    cp(ev, stg_f[ss_][:, 0:32], psum[:, bk, 0:32], [("ps", bk)], [("sf", ss_)])
                                    dma("sp", dtraw[sg].ap()[tix * 128:(tix + 1) * 128, :], stg_f[ss_][:, 0:32], [("sf", ss_)], [], f"sf{ss_}")
                barrier()
                stage('phaseA')
                for sg in range(2):
                    for hh in range(NH):
                        Rg = RS[sg]
                        coll_ap(kT_loc[sg].ap()[hh * 128:(hh + 1) * 128, :], kT_all[sg].ap()[hh * Rg * 128:(hh + 1) * Rg * 128, :],
                                GROUPS[sg], "cc")
                        coll_ap(v_loc[sg].ap()[hh * 128:(hh + 1) * 128, :], v_all[sg].ap()[hh * Rg * 128:(hh + 1) * Rg * 128, :],
                                GROUPS[sg], "cc")
                    coll(edge_loc[sg], edge_all[sg], GROUPS[sg], "cc")
                barrier()

                stage('coll')
                for sg in range(2):
                    R = RS[sg]
                    E_t = AR.alloc((R, 64), F32)
                    H_t = AR.alloc((16, 4), F32)
                    dma("sp", E_t, edge_all[sg].ap().rearrange("(r p) e -> p r e", p=128), (), ["E"], "E")
                    for side, (nm, c0, d0) in enumerate((("selL", 2, 0), ("selR", 0, 2))):
                        for r in range(R):
                            src = E_t[:, r, :].rearrange("p (cb e) -> p cb e", e=4)[:, :, c0:c0 + 2]
                            dst = H_t[:, :, d0:d0 + 2]
                            if r == 0:
                                ts("dve", dst, src, pcol(nm, sg, r), None, ALU.mult, None, ["E", "pc"], [("H", side)])
                            else:
                                stt(dst, src, pcol(nm, sg, r), dst, ALU.mult, ALU.add, ["E", "pc", ("H", side)], [("H", side)])
                    xactT = AR.alloc((8, T), BF16)
                    BT = AR.alloc((4, T), BF16)
                    CT = AR.alloc((4, T), BF16)
                    dt_all = AR.alloc((NT, 32), F32)
                    a_all = AR.alloc((NT, 32), F32)
                    cs_tok = AR.alloc((NT, 32), F32)
                    tot_bc = AR.alloc((NT, 32), F32)
                    dtw = AR.alloc((NT, 32), F32)
                    cd = AR.alloc((NT, 32), F32)
                    eoff = AR.alloc((NT, 32), F32)
                    tdec = AR.alloc((32,), F32)
                    M1 = AR.off
                    cin = [AR.alloc((T + 4,), F32) for _ in range(2)]
                    cacc = [AR.alloc((T,), F32) for _ in range(2)]
                    for cbk in range(16):
                        sl = cbk % 2
                        dma("sp", cin[sl][:, 2:T + 2], xbcT[sg].ap()[cbk * 128:(cbk + 1) * 128, :], (), [("cin", sl)], f"cin{sl}")
                        cp("pool", cin[sl][:, 0:2], H_t[:, cbk, 0:2], [("H", 0)], [("cinh", sl)])
                        cp("pool", cin[sl][:, T + 2:T + 4], H_t[:, cbk, 2:4], [("H", 1)], [("cinh", sl)])
                        ts("dve", cacc[sl], cin[sl][:, 0:T], cw[:, cbk, 0:1], None, ALU.mult, None,
                           [("cin", sl), ("cinh", sl), "cw"], [("cacc", sl)])
                        for k in range(1, 5):
                            stt(cacc[sl], cin[sl][:, k:k + T], cw[:, cbk, k:k + 1], cacc[sl], ALU.mult, ALU.add,
                                [("cin", sl), ("cinh", sl), "cw", ("cacc", sl)], [("cacc", sl)])
                        if cbk < 8:
                            dst = xactT[:, cbk, :]
                        elif cbk < 12:
                            dst = BT[:, cbk - 8, :]
                        else:
                            dst = CT[:, cbk - 12, :]
                        act(dst, cacc[sl], AF.Silu, [("cacc", sl), "cbias"], ["xbc"], bias=cbias[:, cbk:cbk + 1])
                    stage('conv')
                    for c4 in range(0, NT, 4):
                        dma("sp", dt_all[:, c4:c4 + 4, :], dtraw[sg].ap()[c4 * 128:(c4 + 4) * 128, :].rearrange("(c p) h -> p c h", p=128), (), ["dt0"], "dt")
                    tt("dve", dt_all, dt_all, dtb_bc.unsqueeze(1).to_broadcast([128, NT, 32]), ALU.add, ["dt0", "dtb"], ["dt1"])
                    act(dt_all, dt_all, AF.Exp, ["dt1"], ["dt2"])
                    act(dt_all, dt_all, AF.Ln, ["dt2"], ["dt"], bias=1.0)
                    tt("dve", a_all, dt_all, A_bc.unsqueeze(1).to_broadcast([128, NT, 32]), ALU.mult, ["dt", "A"], ["a"])
                    pcsF = psum[:, 0, :].rearrange("p (c h) -> p c h", h=16)[:, 0:NT, :]
                    pcsB = psum[:, 1, :].rearrange("p (c h) -> p c h", h=16)[:, 0:NT, :]
                    mm(pcsF, Umat, a_all[:, :, 0:16], True, True, ["a", "cst"], [("ps", 0)])
                    mm(pcsB, Lmat, a_all[:, :, 16:32], True, True, ["a", "cst"], [("ps", 1)])
                    ptot = psum[:, 2, :].rearrange("p (c h) -> p c h", h=32)[:, 0:NT, :]
                    mm(ptot, ones_f, a_all, True, True, ["a", "cst"], [("ps", 2)])
                    cp("dve", cs_tok[:, :, 0:16], pcsF, [("ps", 0)], ["cs"])
                    cp("dve", cs_tok[:, :, 16:32], pcsB, [("ps", 1)], ["cs"])
                    cp("act", tot_bc, ptot, [("ps", 2)], ["tot"])
                    tt("dve", dtw, tot_bc, cs_tok, ALU.subtract, ["tot", "cs"], ["dtw0"])
                    act(dtw, dtw, AF.Exp, ["dtw0"], ["dtw1"])
                    tt("dve", dtw, dtw, dt_all, ALU.mult, ["dtw1", "dt"], ["dtw"])
                    act(cd, tot_bc, AF.Exp, ["tot"], ["cd"])
                    act(eoff, cs_tok, AF.Exp, ["cs"], ["eoff"])
                    S.add("dve", lambda e, tdec=tdec, tot_bc=tot_bc: e.tensor_reduce(
                        out=tdec, in_=tot_bc.rearrange("p c h -> p h c"), axis=AX.X, op=ALU.add), ["tot"], ["tdec0"])
                    act(tdec, tdec, AF.Exp, ["tdec0"], ["tdec"])

                    S.barrier(markers)
                    AR.reset(M1)
                    stage('dtpre')
                    xdtw = [AR.alloc((2, 16, 64), BF16) for _ in range(2)]
                    Btok = [AR.alloc((4, 128), BF16) for _ in range(2)]
                    sstg = [AR.alloc((2048,), F32) for _ in range(2)]
                    pX = psum[:, 2, :].bitcast(BF16).rearrange("p (h e) -> p h e", h=16)
                    pB = psum[:, 3, :].bitcast(BF16)[:, 0:512].rearrange("p (g n) -> p g n", g=4)
                    for c in range(NT):
                        sl = c % 2
                        for cbk in range(8):
                            tr(pX[:, 2 * cbk:2 * cbk + 2, :].rearrange("p a b -> p (a b)"), xactT[:, cbk, c * 128:(c + 1) * 128], identb,
                               ["xbc", "identb"], [("ps", 2)])
                        for g in range(4):
                            tr(pB[:, g, :], BT[:, g, c * 128:(c + 1) * 128], identb, ["xbc", "identb"], [("ps", 3)])
                        for d in range(2):
                            tt("dve", xdtw[sl][:, d], pX, dtw[:, c, d * 16:(d + 1) * 16].unsqueeze(2).to_broadcast([128, 16, 64]),
                               ALU.mult, [("ps", 2), "dtw"], [("xdtw", sl)])
                        cp("act", Btok[sl], pB, [("ps", 3)], [("Btok", sl)])
                        for d in range(2):
                            for g in range(4):
                                mm(psum[:, 4 + 2 * d + g // 2, (g % 2) * 256:(g % 2 + 1) * 256], Btok[sl][:, g, :],
                                   xdtw[sl][:, d, 4 * g:4 * g + 4, :].rearrange("p a b -> p (a b)"),
                                   True, True, [("Btok", sl), ("xdtw", sl)], [("ps", 4 + 2 * d + g // 2)])
                        cp("act", sstg[sl][:, 0:1024], pbank(4, 2), [("ps", 4), ("ps", 5)], [("sstg", sl)])
                        cp("dve", sstg[sl][:, 1024:2048], pbank(6, 2), [("ps", 6), ("ps", 7)], [("sstg", sl)])
                        dma("sp", Sscr[sg].ap()[c * 128:(c + 1) * 128, :], sstg[sl], [("sstg", sl)], [], f"sstg{sl}")
                    S.barrier(markers)
                    AR.reset(M1)

                    stage('pass1')
                    hst = AR.alloc((2, 1024), F32)
                    sld = [AR.alloc((2048,), F32) for _ in range(2)]
                    ms("pool", hst, 0.0, [("hst", 0), ("hst", 1)])
                    for i in range(NT):
                        sl = i % 2
                        for d in range(2):
                            c = i if d == 0 else NT - 1 - i
                            dma("sp", sld[sl][:, d * 1024:(d + 1) * 1024], Sscr[sg].ap()[c * 128:(c + 1) * 128, d * 1024:(d + 1) * 1024],
                                (), [("sld", sl, d)], f"sld{sl}{d}")
                            hv = hst[:, d, :].rearrange("p (h e) -> p h e", h=16)
                            tt("dve", hv, hv, cd[:, c, d * 16:(d + 1) * 16].unsqueeze(2).to_broadcast([128, 16, 64]), ALU.mult,
                               ["cd", ("hst", d)], [("hst", d)])
                            tt("dve" if d == 0 else "pool", hst[:, d, :], hst[:, d, :], sld[sl][:, d * 1024:(d + 1) * 1024], ALU.add,
                               [("hst", d), ("sld", sl, d)], [("hst", d)])
                    dma("sp", st_loc[sg].ap()[:, 0:2048], hst.rearrange("p a b -> p (a b)"), [("hst", 0), ("hst", 1)], [], "hst")
                    dma("sp", st_loc[sg].ap()[:, 2048:2080], tdec, ["tdec"], [], "hst")
                    S.barrier(markers)
                    coll(st_loc[sg], st_all[sg], GROUPS[sg], "cc")
                    S.barrier(markers)

                    stage('pass1b')
                    hin = AR.alloc((2, 1024), F32)
                    sall = AR.alloc((R, 1024), F32)
                    dall = AR.alloc((R, 32), F32)
                    dmk = AR.alloc((16,), F32)
                    ms("pool", hin, 0.0, [("hin", 0), ("hin", 1)])
                    dma("sp", dall, st_all[sg].ap().rearrange("(r p) e -> p r e", p=128)[:, :, 2048:2080], (), ["dall"], "dall")
                    for d in range(2):
                        dma("sp", sall, st_all[sg].ap().rearrange("(r p) e -> p r e", p=128)[:, :, d * 1024:(d + 1) * 1024], (), ["sall"], "sall")
                        order = range(R) if d == 0 else range(R - 1, -1, -1)
                        nm = "stF" if d == 0 else "stB"
                        hv = hin[:, d, :].rearrange("p (h e) -> p h e", h=16)
                        for r in order:
                            ts("dve", dmk, dall[:, r, d * 16:(d + 1) * 16], -1.0, pcol(nm, sg, r), ALU.add, ALU.mult, ["dall", "pc"], ["dmk"])
                            ts("dve", dmk, dmk, 1.0, None, ALU.add, None, ["dmk"], ["dmk"])
                            tt("dve", hv, hv, dmk.unsqueeze(2).to_broadcast([128, 16, 64]), ALU.mult, ["dmk", ("hin", d)], [("hin", d)])
                            stt(hin[:, d, :], sall[:, r, :], pcol(nm, sg, r), hin[:, d, :], ALU.mult, ALU.add,
                                ["sall", "pc", ("hin", d)], [("hin", d)])
                    pstg = [AR.alloc((2, 1024), BF16) for _ in range(2)]
                    for i in range(NT):
                        sl = i % 2
                        for d in range(2):
                            c = i if d == 0 else NT - 1 - i
                            dma("sp", sld[sl][:, d * 1024:(d + 1) * 1024], Sscr[sg].ap()[c * 128:(c + 1) * 128, d * 1024:(d + 1) * 1024],
                                (), [("sld", sl, d)], f"sld{sl}{d}")
                            cp("act", pstg[sl][:, d, :], hin[:, d, :], [("hin", d)], [("pstg", sl, d)])
                            dma("sp", Pscr[sg].ap()[c * 128:(c + 1) * 128, d * 1024:(d + 1) * 1024], pstg[sl][:, d, :],
                                [("pstg", sl, d)], [], f"pstg{sl}{d}")
                            hv = hin[:, d, :].rearrange("p (h e) -> p h e", h=16)
                            tt("dve", hv, hv, cd[:, c, d * 16:(d + 1) * 16].unsqueeze(2).to_broadcast([128, 16, 64]), ALU.mult,
                               ["cd", ("hin", d)], [("hin", d)])
                            tt("dve" if d == 0 else "pool", hin[:, d, :], hin[:, d, :], sld[sl][:, d * 1024:(d + 1) * 1024], ALU.add,
                               [("hin", d), ("sld", sl, d)], [("hin", d)])
                    S.barrier(markers)
                    AR.reset(M1)

                    stage('pass1c')
                    snw = AR.alloc((1024,), F32)
                    dma("sp", snw, bc_dram(ssm_norm, L * 1024, 1024), (), ["snw"], "snw")
                    xdt = [AR.alloc((2, 16, 64), BF16) for _ in range(2)]
                    cbm = [AR.alloc((2, 4, 128), F32) for _ in range(2)]
                    et = [AR.alloc((4, 128), F32) for _ in range(2)]
                    MT = [AR.alloc((4, 128), BF16) for _ in range(2)]
                    prv = [AR.alloc((2, 1024), BF16) for _ in range(2)]
                    zt = [AR.alloc((1024,), F32) for _ in range(2)]
                    yacc = AR.alloc((1024,), F32)
                    ytmp = AR.alloc((1024,), F32)
                    sout = [AR.alloc((1024,), BF16) for _ in range(2)]
                    sst = AR.alloc((8,), F32)
                    pCB = psum[:, 3, :].rearrange("p (g l) -> p g l", g=4)
                    gcnt = 0
                    for c in range(NT):
                        sl = c % 2
                        csl = slice(c * 128, (c + 1) * 128)
                        dma("sp", prv[sl].rearrange("p a b -> p (a b)"), Pscr[sg].ap()[csl, :], (), [("prv", sl)], f"prv{sl}")
                        dma("sp", zt[sl], zscr[sg].ap()[csl, :], (), [("zt", sl)], f"zt{sl}")
                        for cbk in range(8):
                            tr(pX[:, 2 * cbk:2 * cbk + 2, :].rearrange("p a b -> p (a b)"), xactT[:, cbk, csl], identb,
                               ["xbc", "identb"], [("ps", 2)])
                        for d in range(2):
                            tt("dve", xdt[sl][:, d], pX, dt_all[:, c, d * 16:(d + 1) * 16].unsqueeze(2).to_broadcast([128, 16, 64]),
                               ALU.mult, [("ps", 2), "dt"], [("xdt", sl)])
                        for g in range(4):
                            mm(pCB[:, g, :], BT[:, g, csl], CT[:, g, csl], True, True, ["xbc"], [("ps", 3)])
                        tt("dve", cbm[sl][:, 0], pCB, Umat.unsqueeze(1).to_broadcast([128, 4, 128]), ALU.mult, [("ps", 3), "cst"], [("cbm", sl)])
                        tt("dve", cbm[sl][:, 1], pCB, Lmat.unsqueeze(1).to_broadcast([128, 4, 128]), ALU.mult, [("ps", 3), "cst"], [("cbm", sl)])
                        first_y = [True, True]
                        for d in range(2):
                            tri = Umat if d == 0 else Lmat
                            for g in range(4):
                                gs = gcnt % 2
                                gcnt += 1
                                pbc = psum[:, gs, :].rearrange("p (j l) -> p j l", j=4)
                                for j in range(4):
                                    hh = d * 16 + 4 * g + j
                                    mm(pbc[:, j, :], a_all[:, c, hh:hh + 1].to_broadcast([128, 128]), tri, True, True,
                                       ["a", "cst"], [("ps", gs)])
                                for j in range(4):
                                    hh = d * 16 + 4 * g + j
                                    ts("dve", et[gs][:, j, :], pbc[:, j, :], cs_tok[:, c, hh:hh + 1], 0.0, ALU.subtract, ALU.min,
                                       [("ps", gs), "cs", ("et", gs)], [("et", gs)])
                                act(et[gs], et[gs], AF.Exp, [("et", gs)], [("et", gs)])
                                tt("pool", MT[gs], et[gs], cbm[sl][:, d, g, :].unsqueeze(1).to_broadcast([128, 4, 128]), ALU.mult,
                                   [("et", gs), ("cbm", sl)], [("MT", gs)])
                                for j in range(4):
                                    h = 4 * g + j
                                    bk = 4 + h // 8
                                    mm(psum[:, bk, (h % 8) * 64:(h % 8 + 1) * 64], MT[gs][:, j, :], xdt[sl][:, d, h, :],
                                       first_y[h // 8], (d == 1 and h % 8 == 7), [("MT", gs), ("xdt", sl)], [("ps", bk)], skip=True)
                                    first_y[h // 8] = False
                        cp("act", yacc, pbank(4, 2), [("ps", 4), ("ps", 5)], ["yacc"])
                        for d in range(2):
                            for g in range(4):
                                mm(psum[:, 6 + g // 2, (g % 2) * 256:(g % 2 + 1) * 256], CT[:, g, csl], prv[sl][:, d, g * 256:(g + 1) * 256],
                                   True, True, ["xbc", ("prv", sl)], [("ps", 6 + g // 2)])
                            tt("dve", ytmp.rearrange("p (h e) -> p h e", h=16), pbank(6, 2).rearrange("p (h e) -> p h e", h=16),
                               eoff[:, c, d * 16:(d + 1) * 16].unsqueeze(2).to_broadcast([128, 16, 64]), ALU.mult,
                               [("ps", 6), ("ps", 7), "eoff", "ytmp"], ["ytmp"])
                            tt("pool", yacc, yacc, ytmp, ALU.add, ["yacc", "ytmp"], ["yacc"])
                        tt("dve", ytmp.rearrange("p (h e) -> p h e", h=16), pX, Dsk_bc.unsqueeze(2).to_broadcast([128, 16, 64]), ALU.mult,
                           [("ps", 2), "Dsk", "ytmp"], ["ytmp"])
                        tt("pool", yacc, yacc, ytmp, ALU.add, ["yacc", "ytmp"], ["yacc"])
                        act(zt[sl], zt[sl], AF.Silu, [("zt", sl)], [("zt", sl)])
                        tt("dve", yacc, yacc, zt[sl], ALU.mult, ["yacc", ("zt", sl)], ["yacc"])
                        for g in range(4):
                            act(ytmp[:, g * 256:(g + 1) * 256], yacc[:, g * 256:(g + 1) * 256], AF.Square, ["yacc", "ytmp"], ["ytmp", "sst"],
                                accum_out=sst[:, g:g + 1])
                        ts("dve", sst[:, 4:8], sst[:, 0:4], 1.0 / 256, EPS, ALU.mult, ALU.add, ["sst"], ["sst2"])
                        rsq(sst[:, 4:8], ["sst2"], ["sst2"])
                        tt("dve", yacc.rearrange("p (g e) -> p g e", g=4), yacc.rearrange("p (g e) -> p g e", g=4),
                           sst[:, 4:8].unsqueeze(2).to_broadcast([128, 4, 256]), ALU.mult, ["yacc", "sst2"], ["yacc"])
                        tt("dve", sout[sl], yacc, snw, ALU.mult, ["yacc", "snw", ("sout", sl)], [("sout", sl)])
                        dma("sp", ao[sg].ap()[csl, 1024:2048], sout[sl], [("sout", sl)], [], f"sout{sl}")
                    barrier()

                stage('ssd')
                for sg in range(2):
                    R = RS[sg]
                    NKT = R * NT
                    NKEY = R * T
                    bK = AR.alloc((R, NH, 2, 128), BF16)
                    bV = AR.alloc((R, NH, 2, 128), BF16)
                    xK = AR.alloc((NH, 2, 128), BF16)
                    xV = AR.alloc((NH, 2, 128), BF16)
                    sel1 = AR.alloc((2,), F32)
                    kview = kT_all[sg].ap().rearrange("(h r d) t -> d r h t", r=R, h=NH)
                    vview = v_all[sg].ap().rearrange("(h r p) (kt e) -> p r h kt e", r=R, h=NH, e=128)
                    for side, cs0, ktile in ((0, 0, 0), (1, T - 128, NT - 1)):
                        for r in range(R):
                            dma("sp", bK[:, r, :, side, :], kview[:, r, :, cs0:cs0 + 128], (), ["bK"], "bK")
                            dma("sp", bV[:, r, :, side, :], vview[:, r, :, ktile, :], (), ["bV"], "bV")
                    for xside, (nm, src_side) in enumerate((("selL", 1), ("selR", 0))):
                        for r in range(R):
                            for (bsrc, xd_, key) in ((bK, xK, "xK"), (bV, xV, "xV")):
                                if r == 0:
                                    ts("dve", xd_[:, :, xside, :], bsrc[:, r, :, src_side, :], pcol(nm, sg, r), None, ALU.mult, None,
                                       ["bK", "bV", "pc"], [(key, xside)])
                                else:
                                    stt(xd_[:, :, xside, :], bsrc[:, r, :, src_side, :], pcol(nm, sg, r), xd_[:, :, xside, :], ALU.mult, ALU.add,
                                        ["bK", "bV", "pc", (key, xside)], [(key, xside)])
                        o = pco[(nm, sg)]
                        red(sel1[:, xside:xside + 1], pc_t[:, o:o + R], ["pc"], ["sel1"])
                    KT = AR.alloc((NKEY,), BF16)
                    Vt = AR.alloc((NKT, 129), BF16)
                    KTx = AR.alloc((T + 256,), BF16)
                    Vx = AR.alloc((NT + 2, 129), BF16)
                    QT = AR.alloc((T,), BF16)
                    Gt = AR.alloc((GW,), F32)
                    bt = AR.alloc((NQB * NKT,), F32)
                    PT = [AR.alloc((2, 512), BF16) for _ in range(3)]
                    dtmp = AR.alloc((2, 512), F32)
                    ostg = [AR.alloc((4, 128), BF16) for _ in range(2)]
                    fin = AR.alloc((4, 128), F32)
                    fsm = AR.alloc((16,), F32)
                    ms("pool", Vt[:, :, 128:129], 1.0, ["Vt"])
                    ms("pool", Vx[:, 1:NT + 1, 128:129], 1.0, ["Vx"])
                    cp("pool", Vx[:, 0, 128:129], sel1[:, 0:1], ["sel1"], ["Vx"])
                    cp("pool", Vx[:, NT + 1, 128:129], sel1[:, 1:2], ["sel1"], ["Vx"])
                    pcnt = 0
                    ocnt = 0
                    for h in range(NH):
                        dma("sp", KT.rearrange("p (r t) -> p r t", r=R), kview[:, :, h, :], (), ["KT"], "KT")
                        for r in range(R):
                            for k4 in range(0, NT, 4):
                                dma("sp", Vt[:, r * NT + k4:r * NT + k4 + 4, 0:128], vview[:, r, h, k4:k4 + 4, :], (), ["Vt"], "Vt")
                        dma("sp", KTx[:, 128:T + 128], kT_loc[sg].ap()[h * 128:(h + 1) * 128, :], (), ["KTx"], "KTx")
                        for k4 in range(0, NT, 4):
                            dma("sp", Vx[:, 1 + k4:5 + k4, 0:128],
                                v_loc[sg].ap()[h * 128:(h + 1) * 128, k4 * 128:(k4 + 4) * 128].rearrange("p (k e) -> p k e", e=128),
                                (), ["Vx"], "Vx")
                        dma("sp", QT, qT[sg].ap()[h * 128:(h + 1) * 128, :], (), ["QT"], "QT")
                        dma("sp", Gt, bass.AP(tensor=gsk.ap().tensor, offset=h * 128 * GVP + 127, ap=[[GVP - 1, 128], [1, GW]]), (), ["Gt"], "Gt")
                        cp("pool", KTx[:, 0:128], xK[:, h, 0, :], [("xK", 0)], ["KTx"])
                        cp("pool", KTx[:, T + 128:T + 256], xK[:, h, 1, :], [("xK", 1)], ["KTx"])
                        cp("pool", Vx[:, 0, 0:128], xV[:, h, 0, :], [("xV", 0)], ["Vx"])
                        cp("pool", Vx[:, NT + 1, 0:128], xV[:, h, 1, :], [("xV", 1)], ["Vx"])
                        oL, oR, oN = pco[("mL", sg)], pco[("mR", sg)], pco[("mN", sg)]
                        nb_ = NQB * NKT
                        ts("dve", bt, pc_t[:, oL:oL + nb_], Gt[:, GW - 1:GW], None, ALU.mult, None, ["pc", "Gt"], ["bt"])
                        stt(bt, pc_t[:, oR:oR + nb_], Gt[:, 0:1], bt, ALU.mult, ALU.add, ["pc", "Gt", "bt"], ["bt"])
                        stt(bt, pc_t[:, oN:oN + nb_], NEG, bt, ALU.mult, ALU.add, ["pc", "bt"], ["bt"])
                        for qb in range(NQB):
                            qsl = slice(qb * 512, (qb + 1) * 512)
                            first = [True, True, True]
                            tiles = [("far", kt) for kt in range(NKT)] + [("near", e) for e in range(4 * qb, 4 * qb + 6)]
                            for tile_i, (kind, kt) in enumerate(tiles):
                                sb_ = (pcnt % 2) * 2
                                ps_ = pcnt % 3
                                pcnt += 1
                                if kind == "far":
                                    ksrc = KT[:, kt * 128:(kt + 1) * 128]
                                    vsrc = Vt[:, kt, :]
                                    kkey, vkey = "KT", "Vt"
                                else:
                                    ksrc = KTx[:, kt * 128:(kt + 1) * 128]
                                    vsrc = Vx[:, kt, :]
                                    kkey, vkey = "KTx", "Vx"
                                for cmp_ in range(2):
                                    mm(psum[:, sb_ + cmp_, :], ksrc[cmp_ * 64:(cmp_ + 1) * 64, :], QT[cmp_ * 64:(cmp_ + 1) * 64, qsl], True, True,
                                       [kkey, "QT"], [("ps", sb_ + cmp_)])
                                sc = psum[:, sb_:sb_ + 2, :]
                                if kind == "far":
                                    i = qb * NKT + kt
                                    act(PT[ps_], sc, AF.Exp, [("ps", sb_), ("ps", sb_ + 1), "bt"], [("PT", ps_)], bias=bt[:, i:i + 1], scale=0.125)
                                else:
                                    o = (kt - 1) - 4 * qb
                                    g0 = C0 - 128 * o
                                    stt(dtmp, sc, 0.125, Gt[:, g0:g0 + 512].unsqueeze(1).to_broadcast([128, 2, 512]), ALU.mult, ALU.add,
                                        [("ps", sb_), ("ps", sb_ + 1), "Gt", "dtmp"], ["dtmp"])
                                    act(PT[ps_], dtmp, AF.Exp, ["dtmp"], [("PT", ps_)])
                                for cmp_ in range(2):
                                    for qs in range(4):
                                        ai = cmp_ * 4 + qs
                                        bk = 4 + ai // 3
                                        mm(psum[:, bk, (ai % 3) * 129:(ai % 3) * 129 + 129], PT[ps_][:, cmp_, qs * 128:(qs + 1) * 128], vsrc,
                                           first[ai // 3], (tile_i == len(tiles) - 1 and ai in (2, 5, 7)), [("PT", ps_), vkey], [("ps", bk)], skip=True)
                                        first[ai // 3] = False
                            os_ = ocnt % 2
                            ocnt += 1

                            def accv(cmp_, qs):
                                ai = cmp_ * 4 + qs
                                return psum[:, 4 + ai // 3, (ai % 3) * 129:(ai % 3) * 129 + 129]
                            accr = [("ps", 4), ("ps", 5), ("ps", 6)]
                            for qs in range(4):
                                recip(fsm[:, qs * 2:qs * 2 + 1], accv(0, qs)[:, 128:129], accr, [("fsm", qs)])
                                recip(fsm[:, qs * 2 + 1:qs * 2 + 2], accv(1, qs)[:, 128:129], accr + [("fsm", qs)], [("fsm", qs)])
                                tt("dve", fsm[:, qs * 2 + 1:qs * 2 + 2], fsm[:, qs * 2 + 1:qs * 2 + 2], lamneg, ALU.mult, [("fsm", qs), "lamneg"], [("fsm", qs)])
                                act(fin[:, qs, :], accv(0, qs)[:, 0:128], AF.Identity, accr + [("fsm", qs)], [("fin", qs)], scale=fsm[:, qs * 2:qs * 2 + 1])
                                stt(fin[:, qs, :], accv(1, qs)[:, 0:128], fsm[:, qs * 2 + 1:qs * 2 + 2], fin[:, qs, :], ALU.mult, ALU.add,
                                    accr + [("fsm", qs), ("fin", qs)], [("fin", qs)])
                                act(dtmp[:, 0, qs * 128:(qs + 1) * 128], fin[:, qs, :], AF.Square, [("fin", qs), "dtmp"], ["dtmp", ("fs2", qs)],
                                    accum_out=fsm[:, 8 + qs:9 + qs])
                                ts("dve", fsm[:, 12 + qs:13 + qs], fsm[:, 8 + qs:9 + qs], 1.0 / 128, EPS, ALU.mult, ALU.add, [("fs2", qs)], [("fs3", qs)])
                                rsq(fsm[:, 12 + qs:13 + qs], [("fs3", qs)], [("fs3", qs)])
                                stt(ostg[os_][:, qs, :], fin[:, qs, :], fsm[:, 12 + qs:13 + qs], anw, ALU.mult, ALU.mult,
                                    [("fin", qs), ("fs3", qs), "anw", ("ostg", os_)], [("ostg", os_)])
                            dma("sp", ao[sg].ap()[qsl, h * 128:(h + 1) * 128].rearrange("(q p) e -> p q e", p=128), ostg[os_],
                                [("ostg", os_)], [], f"ostg{os_}")
                    barrier()

                stage('attn')
                wo = AR.alloc((NK, D), BF16)
                for i in range(4):
                    dma("pool", wo[:, i * 4:(i + 1) * 4, :],
                        w_out.ap()[L][i * 512:(i + 1) * 512, :].rearrange("(k p) n -> p k n", p=128), (), ["wo"], "wo")
                nw2 = AR.alloc((D,), F32)
                dma("sp", nw2, bc_dram(post_norm_mix, L * D, D), (), ["nw2"], "nw")
                aot = [AR.alloc((D,), BF16) for _ in range(2)]
                aT = [AR.alloc((NK, 128), BF16) for _ in range(2)]
                xr = [AR.alloc((D,), F32) for _ in range(2)]
                mt = [AR.alloc((D,), F32) for _ in range(2)]
                est = AR.alloc((16,), F32)
                junk = AR.alloc((512,), BF16)
                for tix in range(2 * NT):
                    sl = tix % 2
                    sg = tix // NT
                    rl = slice((tix % NT) * 128, (tix % NT + 1) * 128)
                    rg = slice(tix * 128, (tix + 1) * 128)
                    dma("sp", aot[sl], ao[sg].ap()[rl, :], (), [("aot", sl)], f"aot{sl}")
                    dma("sp", xr[sl], xsrc.ap()[rg, :], (), [("xr", sl)], f"xr{sl}")
                    for k in range(NK):
                        tr(pT[:, k, :], aot[sl][:, k * 128:(k + 1) * 128], identb, [("aot", sl), "identb"], ["pT"])
                    cp("act", aT[sl], pT, ["pT"], [("aT", sl)])
                    for ob in range(4):
                        for k in range(NK):
                            mm(psum[:, 4 + ob, :], aT[sl][:, k, :], wo[:, k, ob * 512:(ob + 1) * 512], k == 0, k == NK - 1,
                               [("aT", sl), "wo"], [("ps", 4 + ob)])
                        act(junk, psum[:, 4 + ob, :], AF.Square, [("ps", 4 + ob), "junk"], ["junk", ("est", sl, ob)],
                            accum_out=est[:, sl * 8 + ob:sl * 8 + ob + 1])
                    red(est[:, sl * 8 + 4:sl * 8 + 5], est[:, sl * 8:sl * 8 + 4], [("est", sl, ob) for ob in range(4)], [("ers", sl)])
                    ts("dve", est[:, sl * 8 + 4:sl * 8 + 5], est[:, sl * 8 + 4:sl * 8 + 5], 1.0 / D, EPS, ALU.mult, ALU.add, [("ers", sl)], [("ers", sl)])
                    rsq(est[:, sl * 8 + 4:sl * 8 + 5], [("ers", sl)], [("ers", sl)])
                    for ob in range(4):
                        stt(mt[sl][:, ob * 512:(ob + 1) * 512], psum[:, 4 + ob, :], est[:, sl * 8 + 4:sl * 8 + 5], nw2[:, ob * 512:(ob + 1) * 512],
                            ALU.mult, ALU.mult, [("ps", 4 + ob), ("ers", sl), "nw2", ("mt", sl)], [("mt", sl)])
                    tt("pool", xr[sl], xr[sl], mt[sl], ALU.add, [("xr", sl), ("mt", sl)], [("xr", sl)])
                    dma("sp", xa.ap()[rg, :], xr[sl], [("xr", sl)], [], f"xr{sl}")
                barrier()

                stage('phaseE')
                nw3 = AR.alloc((D,), F32)
                nw4 = AR.alloc((D,), F32)
                dma("sp", nw3, bc_dram(pre_norm_mlp, L * D, D), (), ["nw3"], "nw")
                dma("sp", nw4, bc_dram(post_norm_mlp, L * D, D), (), ["nw4"], "nw")
                NTB = TBLK // 128
                hT2 = AR.alloc((NK, TBLK), BF16)
                macc = AR.alloc((NTB, D), F32)
                wu = [AR.alloc((NK, 512), BF16) for _ in range(2)]
                wd = [AR.alloc((4, D), BF16) for _ in range(2)]
                uT = [AR.alloc((4, TBLK), BF16) for _ in range(2)]
                ur = [AR.alloc((512,), F32) for _ in range(2)]
                xt2 = [AR.alloc((D,), F32) for _ in range(2)]
                hb2 = [AR.alloc((D,), BF16) for _ in range(2)]
                fst = AR.alloc((16,), F32)
                wc2 = 0
                ucnt = 0
                for blk in range(TOK // TBLK):
                    b0 = blk * TBLK
                    for tix in range(NTB):
                        sl = tix % 2
                        rg = slice(b0 + tix * 128, b0 + (tix + 1) * 128)
                        dma("sp", xt2[sl], xa.ap()[rg, :], (), [("xt2", sl)], f"xt2{sl}")
                        act(hb2[sl], xt2[sl], AF.Square, [("xt2", sl)], [("hb2", sl), ("fss", sl)], accum_out=fst[:, sl:sl + 1])
                        ts("dve", fst[:, 2 + sl:3 + sl], fst[:, sl:sl + 1], 1.0 / D, EPS, ALU.mult, ALU.add, [("fss", sl)], [("frs", sl)])
                        rsq(fst[:, 2 + sl:3 + sl], [("frs", sl)], [("frs", sl)])
                        stt(hb2[sl], xt2[sl], fst[:, 2 + sl:3 + sl], nw3, ALU.mult, ALU.mult, [("xt2", sl), ("frs", sl), "nw3", ("hb2", sl)], [("hb2", sl)])
                        for k in range(NK):
                            tr(pT[:, k, :], hb2[sl][:, k * 128:(k + 1) * 128], identb, [("hb2", sl), "identb"], ["pT"])
                        cp("act" if tix % 2 else "dve", hT2[:, :, tix * 128:(tix + 1) * 128], pT, ["pT"], ["hT2"])
                    for fb in range(DFF // 512):
                        wsl = wc2 % 2
                        wc2 += 1
                        dma("pool", wu[wsl], w_up.ap()[L][:, fb * 512:(fb + 1) * 512].rearrange("(k p) n -> p k n", p=128),
                            (), [("wu", wsl)], f"wu{wsl}")
                        dma("pool", wd[wsl], w_down.ap()[L][fb * 512:(fb + 1) * 512, :].rearrange("(k p) n -> p k n", p=128),
                            (), [("wd", wsl)], f"wd{wsl}")
                        usl = fb % 2
                        for fc in range(4):
                            for tb in range(TBLK // 512):
                                bk = 2 + ucnt % 2
                                us = ucnt % 2
                                ucnt += 1
                                for k in range(NK):
                                    mm(psum[:, bk, :], wu[wsl][:, k, fc * 128:(fc + 1) * 128], hT2[:, k, tb * 512:(tb + 1) * 512], k == 0, k == NK - 1,
                                       [("wu", wsl), "hT2"], [("ps", bk)])
                                act(ur[us], psum[:, bk, :], AF.Relu, [("ps", bk)], [("ur", us)])
                                tt("pool", uT[usl][:, fc, tb * 512:(tb + 1) * 512], ur[us], ur[us], ALU.mult, [("ur", us)], [("uT", usl)])
                        for tix in range(NTB):
                            for ob in range(4):
                                bk = 4 + ob
                                for fc in range(4):
                                    mm(psum[:, bk, :], uT[usl][:, fc, tix * 128:(tix + 1) * 128], wd[wsl][:, fc, ob * 512:(ob + 1) * 512], fc == 0, fc == 3,
                                       [("uT", usl), ("wd", wsl)], [("ps", bk)])
                                dst = macc[:, tix, ob * 512:(ob + 1) * 512]
                                if fb == 0:
                                    cp("dve", dst, psum[:, bk, :], [("ps", bk)], [("macc", tix)])
                                else:
                                    tt("dve", dst, dst, psum[:, bk, :], ALU.add, [("ps", bk), ("macc", tix)], [("macc", tix)])
                    for tix in range(NTB):
                        sl = tix % 2
                        rg = slice(b0 + tix * 128, b0 + (tix + 1) * 128)
                        dma("sp", xt2[sl], xa.ap()[rg, :], (), [("xt2", sl)], f"xt2{sl}")
                        act(hb2[sl], macc[:, tix, :], AF.Square, [("macc", tix)], [("hb2", sl), ("fss", sl)], accum_out=fst[:, 4 + sl:5 + sl])
                        ts("dve", fst[:, 6 + sl:7 + sl], fst[:, 4 + sl:5 + sl], 1.0 / D, EPS, ALU.mult, ALU.add, [("fss", sl)], [("frs", sl)])
                        rsq(fst[:, 6 + sl:7 + sl], [("frs", sl)], [("frs", sl)])
                        stt(macc[:, tix, :], macc[:, tix, :], fst[:, 6 + sl:7 + sl], nw4, ALU.mult, ALU.mult, [("macc", tix), ("frs", sl), "nw4"], [("macc", tix)])
                        tt("pool", xt2[sl], xt2[sl], macc[:, tix, :], ALU.add, [("xt2", sl), ("macc", tix)], [("xt2", sl)])
                        dma("sp", xdst.ap()[rg, :], xt2[sl], [("xt2", sl)], [], f"xt2{sl}")
                barrier()
                xsrc = xb

        except StopBuild:
            S.barrier(markers)
        S.emit(final_wait_sems=list(S.dma_total.keys()))
    return nc


_CACHE = {}


def host_consts():
    cst = np.concatenate([np.eye(128), np.triu(np.ones((128, 128))), np.tril(np.ones((128, 128))), np.ones((128, 128))],
                         axis=1).astype(np.float32)
    rel = (C0 + 127) - np.arange(GV)
    b = rel_bucket_np(rel)
    relR = np.zeros((32, GVP), np.float32)
    relR[b, np.arange(GV)] = 1.0
    return cst, relR


def run(inputs, T, debug=False, stop=None):
    key = (T, debug, stop)
    if key not in _CACHE:
        _CACHE[key] = build_program(T, debug, stop)
    nc = _CACHE[key]
    cst, relR = host_consts()
    xp = np.ascontiguousarray(inputs["x_prompt"], dtype=np.float32)
    xs = np.ascontiguousarray(inputs["x_sample"], dtype=np.float32)
    shared = {k: np.ascontiguousarray(np.asarray(v), dtype=np.float32) for k, v in inputs.items()
              if k not in ("x_prompt", "x_sample")}
    in_maps = []
    for c in range(8):
        xin = np.concatenate([xp[0, c * T:(c + 1) * T], xs[c // 2, (c % 2) * T:(c % 2 + 1) * T]], axis=0)
        m = dict(shared)
        m.update(xin=xin, pc=pc_values(T, c), cst=cst, relR=relR)
        in_maps.append(m)
    res = run_bass_kernel_spmd(nc, in_maps, core_ids=list(range(8)))
    yp = np.concatenate([res.results[c]["yout"][0:T] for c in range(8)], axis=0)[None]
    ys = np.stack([np.concatenate([res.results[2 * s]["yout"][T:2 * T], res.results[2 * s + 1]["yout"][T:2 * T]], axis=0)
                   for s in range(4)], axis=0)
    return (yp.astype(np.float32), ys.astype(np.float32)), res


def kernel(**inputs):
    T = inputs["x_prompt"].shape[1] // 8
    assert inputs["x_sample"].shape[1] == 2 * T and inputs["x_sample"].shape[0] == 4
    (yp, ys), _ = run(inputs, T)
    return yp, ys
```
